# Optimizing a Trainium2 kernel written in Bass

```python
import math
import jax
import jax.numpy as jnp
from jax import lax
import numpy as np

D_MODEL = 2048
BATCH = 2
SEQ = 8192
DEPTH = 4
DEC_BATCH = 8
DEC_SEQ = 16
PAST_LEN = 2048

CHUNK = 64
HEAD_DIM = 64
SWA_Q_HEADS = 16
SWA_KV_HEADS = 4
SWA_GROUP = SWA_Q_HEADS // SWA_KV_HEADS
SWA_WIDTH = SWA_Q_HEADS * HEAD_DIM
WINDOW = 128
WINDOW_CHUNKS = WINDOW // CHUNK
REL_BUCKETS = 32
REL_MAX_DIST = 128
GMLP_GROUPS = 4
GMLP_GROUP_DIM = 128
GMLP_WIDTH = GMLP_GROUPS * GMLP_GROUP_DIM
GMLP_CHUNK = 128
GLA_HEADS = 4
GLA_DK = 64
GLA_DV = 128
GLA_WIDTH = GLA_HEADS * GLA_DV
GLA_GATE_RANK = 16
GLA_GATE_TEMP = 16.0
MIX_WIDTH = SWA_WIDTH + GMLP_WIDTH + GLA_WIDTH
D_FF = 4 * D_MODEL
PLE_DIM = 256
NORM_EPS = 1e-6
IN_SIZES = (SWA_WIDTH, SWA_KV_HEADS * HEAD_DIM, SWA_KV_HEADS * HEAD_DIM,
            GMLP_WIDTH, GMLP_WIDTH,
            GLA_HEADS * GLA_DK, GLA_HEADS * GLA_DK, GLA_WIDTH, GLA_GATE_RANK, GLA_WIDTH)
IN_WIDTH = sum(IN_SIZES)

kernel_name = 'hymba_swa_gmlp_gla_stream_step'


def rms_norm(x, g):
    xf = x.astype(jnp.float32)
    y = xf * lax.rsqrt(jnp.mean(xf * xf, axis=-1, keepdims=True) + NORM_EPS)
    return (y * g.astype(jnp.float32)).astype(x.dtype)


def layer_norm(x, g):
    xf = x.astype(jnp.float32)
    xc = xf - jnp.mean(xf, axis=-1, keepdims=True)
    y = xc * lax.rsqrt(jnp.mean(xc * xc, axis=-1, keepdims=True) + NORM_EPS)
    return (y * g.astype(jnp.float32)).astype(x.dtype)


def t5_bucket(rel):
    half = REL_BUCKETS // 2
    max_exact = half // 2
    n = -rel
    ret = jnp.where(n < 0, half, 0)
    n = jnp.abs(n)
    nf = jnp.maximum(n, 1).astype(jnp.float32)
    large = max_exact + (jnp.log(nf / max_exact) / math.log(REL_MAX_DIST / max_exact)
                         * (half - max_exact)).astype(jnp.int32)
    large = jnp.minimum(large, half - 1)
    return ret + jnp.where(n < max_exact, n, large)


def swa_mixer(aq, ak, av, past_k, past_v, rel_bias, sinks):
    B, L, _ = aq.shape
    q = aq.reshape(B, L, SWA_KV_HEADS, SWA_GROUP, HEAD_DIM)
    k = ak.reshape(B, L, SWA_KV_HEADS, HEAD_DIM)
    v = av.reshape(B, L, SWA_KV_HEADS, HEAD_DIM)
    if past_k is None:
        n_blk = L // CHUNK
        pad = ((0, 0), (WINDOW, 0), (0, 0), (0, 0))
        kp = jnp.pad(k, pad).reshape(B, n_blk + WINDOW_CHUNKS, CHUNK, SWA_KV_HEADS, HEAD_DIM)
        vp = jnp.pad(v, pad).reshape(B, n_blk + WINDOW_CHUNKS, CHUNK, SWA_KV_HEADS, HEAD_DIM)
        kb = jnp.concatenate([kp[:, j:j + n_blk] for j in range(WINDOW_CHUNKS + 1)], axis=2)
        vb = jnp.concatenate([vp[:, j:j + n_blk] for j in range(WINDOW_CHUNKS + 1)], axis=2)
        qb = q.reshape(B, n_blk, CHUNK, SWA_KV_HEADS, SWA_GROUP, HEAD_DIM)
        key_pos = (jnp.arange(n_blk)[:, None] * CHUNK - WINDOW
                   + jnp.arange(WINDOW + CHUNK)[None, :])
        valid = key_pos >= 0
        n_past = WINDOW
        new_k, new_v = k[:, L - WINDOW:], v[:, L - WINDOW:]
    else:
        n_past = past_k.shape[1]
        kb = jnp.concatenate([past_k.astype(k.dtype), k], axis=1)[:, None]
        vb = jnp.concatenate([past_v.astype(v.dtype), v], axis=1)[:, None]
        qb = q[:, None]
        valid = jnp.ones((1, n_past + L), dtype=bool)
        new_k, new_v = k, v
    lq, lk = qb.shape[2], kb.shape[2]
    rel = jnp.arange(lk)[None, :] - n_past - jnp.arange(lq)[:, None]
    bias = jnp.transpose(rel_bias[t5_bucket(rel)], (2, 0, 1)).reshape(
        SWA_KV_HEADS, SWA_GROUP, lq, lk).astype(jnp.float32)
    s = jnp.einsum('bnqhgd,bnkhd->bnhgqk', qb, kb,
                   preferred_element_type=jnp.float32) * (HEAD_DIM ** -0.5)
    s = jnp.where(valid[None, :, None, None, None, :], s + bias, -jnp.inf)
    sink = sinks.astype(jnp.float32).reshape(SWA_KV_HEADS, SWA_GROUP)[None, None, :, :, None, None]
    m = jnp.maximum(jnp.max(s, axis=-1, keepdims=True), sink)
    e = jnp.exp(s - m)
    p = e / (jnp.sum(e, axis=-1, keepdims=True) + jnp.exp(sink - m))
    o = jnp.einsum('bnhgqk,bnkhd->bnqhgd', p.astype(vb.dtype), vb)
    return o.reshape(B, L, SWA_WIDTH), new_k, new_v


def gmlp_spatial(u, v, w_s, b_s):
    B, L, _ = u.shape
    c = min(L, GMLP_CHUNK)
    n = L // c
    pos = jnp.arange(c)
    mask = (pos[:, None] // CHUNK) >= (pos[None, :] // CHUNK)
    w = jnp.where(mask, w_s[:, :c, :c], 0)
    vb = v.reshape(B, n, c, GMLP_GROUPS, GMLP_GROUP_DIM)
    sv = (jnp.einsum('gnm,bcmgd->bcngd', w.astype(v.dtype), vb)
          + b_s[:, :c].T[None, None, :, :, None])
    return u * sv.reshape(B, L, GMLP_WIDTH)


def gla_chunk_step(s, blk):
    q, k, v, la = blk
    c = q.shape[2]
    b = jnp.cumsum(la, axis=2)
    causal = jnp.tril(jnp.ones((c, c), dtype=bool))
    diff = b[:, :, :, None, :] - b[:, :, None, :, :]
    decay = jnp.exp(jnp.where(causal[None, None, :, :, None], diff, -jnp.inf))
    att = jnp.einsum('bhtd,bhsd,bhtsd->bhts', q, k, decay)
    o = (jnp.einsum('bhts,bhsv->bhtv', att, v)
         + jnp.einsum('bhtd,bhdv->bhtv', q * jnp.exp(b), s))
    b_last = b[:, :, -1:, :]
    s_new = (jnp.exp(b_last[:, :, 0, :])[..., None] * s
             + jnp.einsum('bhsd,bhsv->bhdv', k * jnp.exp(b_last - b), v))
    return s_new, o


def gla_mixer(cq, ck, cv, cg, co, s0, w_gate, b_gate, g_onorm):
    B, L, _ = cq.shape
    log_a = jax.nn.log_sigmoid((cg @ w_gate + b_gate).astype(jnp.float32)) / GLA_GATE_TEMP
    c = min(L, CHUNK)
    n = L // c

    def blocks(t, d):
        t = t.astype(jnp.float32).reshape(B, n, c, GLA_HEADS, d)
        return jnp.transpose(t, (1, 0, 3, 2, 4))

    if s0 is None:
        s0 = jnp.zeros((B, GLA_HEADS, GLA_DK, GLA_DV), jnp.float32)
    s_fin, o = lax.scan(gla_chunk_step, s0.astype(jnp.float32),
                        (blocks(cq * GLA_DK ** -0.5, GLA_DK), blocks(ck, GLA_DK),
                         blocks(cv, GLA_DV), blocks(log_a, GLA_DK)))
    o = jnp.transpose(o, (1, 0, 3, 2, 4)).reshape(B, L, GLA_HEADS, GLA_DV)
    o = rms_norm(o, g_onorm.reshape(GLA_HEADS, GLA_DV)).reshape(B, L, GLA_WIDTH)
    return o.astype(cq.dtype) * jax.nn.silu(co), s_fin


def trunk_layer(x, pe, past_k, past_v, gla_s0, w_in, w_gate, b_gate, rel_bias, sinks,
                w_spatial, b_spatial, g_vnorm, g_onorm, w_out, g_mix_pre, g_mix_post,
                g_ffn_pre, g_ffn_post, w_up, w_down, w_ple, w_ple_gate):
    h = rms_norm(x, g_mix_pre)
    z = h @ w_in
    aq, ak, av, bu, bv, cq, ck, cv, cg, co = jnp.split(
        z, np.cumsum(IN_SIZES)[:-1].tolist(), axis=-1)
    o_a, new_k, new_v = swa_mixer(aq, ak, av, past_k, past_v, rel_bias, sinks)
    v_n = layer_norm(jax.nn.gelu(bv), g_vnorm)
    o_b = gmlp_spatial(jax.nn.gelu(bu), v_n, w_spatial, b_spatial)
    o_c, s_fin = gla_mixer(cq, ck, cv, cg, co, gla_s0, w_gate, b_gate, g_onorm)
    mix = jnp.concatenate([o_a, o_b, o_c], axis=-1) @ w_out
    x = x + rms_norm(mix, g_mix_post)
    h = rms_norm(x, g_ffn_pre)
    f = jnp.square(jax.nn.relu(h @ w_up)) @ w_down
    x = x + rms_norm(f, g_ffn_post)
    x = x + (pe @ w_ple) * jax.nn.sigmoid(x @ w_ple_gate)
    return x, new_k, new_v, s_fin, v_n


def setup_inputs(seed: int = 0) -> dict:
    key = jax.random.key(seed)
    ks = jax.random.split(key, 26)

    def nrm(k, shape, scale):
        return scale * jax.random.normal(k, shape, jnp.float32)

    def gain(k, shape):
        return 1.0 + 0.05 * jax.random.normal(k, shape, jnp.float32)

    win_cache = min(WINDOW, PAST_LEN)
    return {
        'x_prompt': nrm(ks[0], (BATCH, SEQ, D_MODEL), 1.0),
        'x_sample': nrm(ks[1], (DEC_BATCH, DEC_SEQ, D_MODEL), 1.0),
        'cache_swa_k': nrm(ks[2], (DEPTH, DEC_BATCH, win_cache, SWA_KV_HEADS, HEAD_DIM), 1.0),
        'cache_swa_v': nrm(ks[3], (DEPTH, DEC_BATCH, win_cache, SWA_KV_HEADS, HEAD_DIM), 1.0),
        'state_gla': nrm(ks[4], (DEPTH, DEC_BATCH, GLA_HEADS, GLA_DK, GLA_DV), 1.0),
        'p_prompt': nrm(ks[5], (DEPTH, BATCH, SEQ, PLE_DIM), 1.0),
        'p_sample': nrm(ks[6], (DEPTH, DEC_BATCH, DEC_SEQ, PLE_DIM), 1.0),
        'w_in': nrm(ks[7], (DEPTH, D_MODEL, IN_WIDTH), D_MODEL ** -0.5),
        'w_gate': nrm(ks[8], (DEPTH, GLA_GATE_RANK, GLA_HEADS * GLA_DK), GLA_GATE_RANK ** -0.5),
        'b_gate': nrm(ks[9], (DEPTH, GLA_HEADS * GLA_DK), 0.1),
        'rel_bias': nrm(ks[10], (REL_BUCKETS, SWA_Q_HEADS), 0.5),
        'attn_sinks': nrm(ks[11], (DEPTH, SWA_Q_HEADS), 0.5),
        'w_spatial': nrm(ks[12], (DEPTH, GMLP_GROUPS, GMLP_CHUNK, GMLP_CHUNK), GMLP_CHUNK ** -0.5),
        'b_spatial': nrm(ks[13], (DEPTH, GMLP_GROUPS, GMLP_CHUNK), 0.1),
        'g_gmlp_vnorm': gain(ks[14], (DEPTH, GMLP_WIDTH)),
        'g_gla_onorm': gain(ks[15], (DEPTH, GLA_WIDTH)),
        'w_out': nrm(ks[16], (DEPTH, MIX_WIDTH, D_MODEL), MIX_WIDTH ** -0.5),
        'g_mix_pre': gain(ks[17], (DEPTH, D_MODEL)),
        'g_mix_post': gain(ks[18], (DEPTH, D_MODEL)),
        'g_ffn_pre': gain(ks[19], (DEPTH, D_MODEL)),
        'g_ffn_post': gain(ks[20], (DEPTH, D_MODEL)),
        'w_up': nrm(ks[21], (DEPTH, D_MODEL, D_FF), D_MODEL ** -0.5),
        'w_down': nrm(ks[22], (DEPTH, D_FF, D_MODEL), D_FF ** -0.5),
        'w_ple': nrm(ks[23], (DEPTH, PLE_DIM, D_MODEL), PLE_DIM ** -0.5),
        'w_ple_gate': nrm(ks[24], (DEPTH, D_MODEL, D_MODEL), D_MODEL ** -0.5),
    }


def reference(x_prompt, x_sample, cache_swa_k, cache_swa_v, state_gla, p_prompt, p_sample,
              w_in, w_gate, b_gate, rel_bias, attn_sinks, w_spatial, b_spatial,
              g_gmlp_vnorm, g_gla_onorm, w_out, g_mix_pre, g_mix_post, g_ffn_pre,
              g_ffn_post, w_up, w_down, w_ple, w_ple_gate):
    y_prompt, y_sample = x_prompt, x_sample
    kp_rows, vp_rows, ks_rows, vs_rows = [], [], [], []
    sp_states, ss_states, gv_rows = [], [], []
    for i in range(DEPTH):
        lw = dict(w_in=w_in[i], w_gate=w_gate[i], b_gate=b_gate[i], rel_bias=rel_bias,
                  sinks=attn_sinks[i], w_spatial=w_spatial[i], b_spatial=b_spatial[i],
                  g_vnorm=g_gmlp_vnorm[i], g_onorm=g_gla_onorm[i], w_out=w_out[i],
                  g_mix_pre=g_mix_pre[i], g_mix_post=g_mix_post[i], g_ffn_pre=g_ffn_pre[i],
                  g_ffn_post=g_ffn_post[i], w_up=w_up[i], w_down=w_down[i],
                  w_ple=w_ple[i], w_ple_gate=w_ple_gate[i])
        y_prompt, k_p, v_p, s_p, _ = trunk_layer(y_prompt, p_prompt[i], None, None, None, **lw)
        y_sample, k_s, v_s, s_s, gv_s = trunk_layer(y_sample, p_sample[i], cache_swa_k[i],
                                                   cache_swa_v[i], state_gla[i], **lw)
        kp_rows.append(k_p)
        vp_rows.append(v_p)
        ks_rows.append(k_s)
        vs_rows.append(v_s)
        sp_states.append(s_p)
        ss_states.append(s_s.astype(state_gla.dtype))
        gv_rows.append(gv_s)
    return (y_prompt, y_sample, jnp.stack(kp_rows), jnp.stack(vp_rows), jnp.stack(ks_rows),
            jnp.stack(vs_rows), jnp.stack(sp_states), jnp.stack(ss_states), jnp.stack(gv_rows))
```

```python
import math
import numpy as np
import concourse.bass as bass
import concourse.mybir as mybir
from concourse.bass_utils import run_bass_kernel_spmd

F32 = mybir.dt.float32
BF16 = mybir.dt.bfloat16
AF = mybir.ActivationFunctionType
ALU = mybir.AluOpType
AX = mybir.AxisListType

NCORES = 8
DEPTH = 4
D = 2048
TP = 2048
TS = 16
NT = TP + TS
TW = 512
CW = 528
DFF = 8192
INW = 4112
EPS = 1e-6
CELL = 528
NCELL = 64
CCW = 1284
NEG = -30000.0
ZK_W = 1792
WDEPTH = DEPTH
STOP = 99
TESTHH = 1


class Buf:
    __slots__ = ("name", "w", "r", "dsem", "dval")

    def __init__(self, name):
        self.name = name
        self.w = None
        self.r = {}
        self.dsem = None
        self.dval = 0


class Eng:
    def __init__(self, name, h, sem):
        self.name = name
        self.h = h
        self.sem = sem
        self.count = 0
        self.seen = {}


class TR:
    def __init__(self, nc):
        self.nc = nc
        self.E = {}
        for name, h in (("pe", nc.tensor), ("act", nc.scalar), ("dve", nc.vector), ("pool", nc.gpsimd), ("sp", nc.sync)):
            self.E[name] = Eng(name, h, nc.alloc_semaphore("sem_" + name))
        self.nsem = 5
        self.out_recs = []
        self.dpool = {}

    def _waits(self, e, reads, writes):
        need = {}

        def req(rec):
            if rec is None:
                return
            s, v = rec
            if e.name == "pe" and s is e.sem:
                return
            if need.get(s, 0) < v:
                need[s] = v

        for b in reads:
            req(b.w)
        for b in writes:
            req(b.w)
            for s, v in b.r.items():
                req((s, v))
        for s, v in need.items():
            if e.seen.get(s, 0) < v:
                e.h.wait_ge(s, v)
                e.seen[s] = v

    def _record(self, rec, reads, writes):
        s, v = rec
        for b in reads:
            if b.r.get(s, 0) < v:
                b.r[s] = v
        for b in writes:
            b.w = rec
            b.r = {}

    def op(self, eng, fn, reads=(), writes=(), inc=True):
        e = self.E[eng]
        self._waits(e, reads, writes)
        ins = fn(e.h)
        if inc:
            e.count += 1
            ins.then_inc(e.sem, 1)
            rec = (e.sem, e.count)
        else:
            rec = (e.sem, e.count + 1)
        self._record(rec, reads, writes)
        return ins

    def dma(self, q, out, in_, reads, writes, owner, is_output=False):
        e = self.E[q]
        self._waits(e, reads, writes)
        pool = self.dpool.setdefault(q, {"sems": [], "vals": [], "ctr": 0, "n": (12 if q == "sp" else 8)})
        k = pool["ctr"] % pool["n"]
        pool["ctr"] += 1
        if k >= len(pool["sems"]):
            pool["sems"].append(self.nc.alloc_semaphore("dsem_%s_%d" % (q, k)))
            pool["vals"].append(0)
            self.nsem += 1
        sem = pool["sems"][k]
        if e.seen.get(sem, 0) < pool["vals"][k]:
            e.h.wait_ge(sem, pool["vals"][k])
            e.seen[sem] = pool["vals"][k]
        ins = e.h.dma_start(out=out, in_=in_)
        pool["vals"][k] += 16
        ins.then_inc(sem, 16)
        rec = (sem, pool["vals"][k])
        self._record(rec, reads, writes)
        if is_output:
            self.out_recs.append(rec)
        return rec

    def finish(self):
        e = self.E["sp"]
        need = {}
        for s, v in self.out_recs:
            if need.get(s, 0) < v:
                need[s] = v
        for s, v in need.items():
            if e.seen.get(s, 0) < v:
                e.h.wait_ge(s, v)
                e.seen[s] = v
        for name in ("pe", "act", "dve", "pool"):
            o = self.E[name]
            if o.count > 0 and e.seen.get(o.sem, 0) < o.count:
                e.h.wait_ge(o.sem, o.count)


class View:
    def __init__(self, ap, bufs):
        self.ap = ap
        self.bufs = list(bufs)


def t5_bucket_np(rel):
    half = 16
    max_exact = 8
    n = -rel
    ret = np.where(n < 0, half, 0)
    n = np.abs(n)
    nf = np.maximum(n, 1).astype(np.float32)
    large = max_exact + (np.log(nf / np.float32(max_exact)) / np.float32(math.log(128 / max_exact))
                         * np.float32(half - max_exact)).astype(np.int32)
    large = np.minimum(large, half - 1)
    return ret + np.where(n < max_exact, n, large)


def host_constants():
    c = {}
    c["ident"] = np.eye(128, dtype=np.float32)
    x = np.arange(383)
    rel = (382 - x) - 255
    bk = t5_bucket_np(rel.astype(np.int32))
    trev = np.zeros((32, 383), np.float32)
    trev[bk, x] = 1.0
    c["trev"] = trev
    sm = np.zeros((128, 256), np.float32)
    sm[0:64, 192:256] = NEG
    sm[64:128, 0:64] = NEG
    c["smask"] = sm
    s = np.arange(64)
    c["triu"] = (-(1.0 / 16.0) * (s[:, None] <= s[None, :])).astype(np.float32)
    c["tris"] = (-(1.0 / 16.0) * (s[:, None] > s[None, :])).astype(np.float32)
    cm = (s[:, None] <= s[None, :]).astype(np.float32)
    c["cmask"] = np.ascontiguousarray(np.broadcast_to(cm[:, None, :], (64, 4, 64))).astype(np.float32)
    n = np.arange(128)
    c["gmask"] = ((n[:, None] // 64) >= (n[None, :] // 64)).astype(np.float32)
    return c


def build(depth=DEPTH):
    nc = bass.Bass("TRN2", target_bir_lowering=False)
    T = TR(nc)

    def din(name, shape, dt=F32):
        return nc.dram_tensor(name, list(shape), dt, kind="ExternalInput").ap()

    def dout(name, shape, dt=F32):
        return nc.dram_tensor(name, list(shape), dt, kind="ExternalOutput").ap()

    def dscr(name, shape, dt=F32):
        return nc.dram_tensor(name, list(shape), dt).ap()

    xp = din("xp", [TP, D]); xs = din("xs", [TS, D])
    pp = din("pp", [DEPTH, TP, 256]); psm = din("psm", [DEPTH, TS, 256])
    ckc = din("ckc", [DEPTH, 128, 256]); cvc = din("cvc", [DEPTH, 128, 256])
    sg = din("sg", [DEPTH, 4, 64, 128])
    w_in = din("w_in", [WDEPTH, D, INW]); w_gate = din("w_gate", [DEPTH, 16, 256]); b_gate = din("b_gate", [DEPTH, 256])
    rel_bias = din("rel_bias", [32, 16]); sinks = din("sinks", [DEPTH, 16])
    w_sp = din("w_sp", [DEPTH, 4, 128, 128]); b_sp = din("b_sp", [DEPTH, 4, 128])
    g_vn = din("g_vn", [DEPTH, 512]); g_on = din("g_on", [DEPTH, 512])
    w_out = din("w_out", [WDEPTH, D, D])
    g1 = din("g1", [DEPTH, D]); g2 = din("g2", [DEPTH, D]); g3 = din("g3", [DEPTH, D]); g4 = din("g4", [DEPTH, D])
    w_up = din("w_up", [WDEPTH, D, DFF]); w_down = din("w_down", [WDEPTH, DFF, D])
    w_ple = din("w_ple", [WDEPTH, 256, D]); w_pg = din("w_pg", [WDEPTH, D, D])
    c_ident = din("ident", [128, 128]); c_trev = din("trev", [32, 383]); c_smask = din("smask", [128, 256])
    c_triu = din("triu", [64, 64]); c_tris = din("tris", [64, 64]); c_cmask = din("cmask", [64, 4, 64])
    c_gmask = din("gmask", [128, 128]); c_sel = din("sel", [128, 16])

    y_p = dout("y_p", [TP, D]); y_s = dout("y_s", [TS, D])
    o_kp = dout("o_kp", [DEPTH, 128, 256]); o_vp = dout("o_vp", [DEPTH, 128, 256])
    o_ks = dout("o_ks", [DEPTH, TS, 256]); o_vs = dout("o_vs", [DEPTH, TS, 256])
    o_stp = dout("o_stp", [DEPTH, 4, 64, 128]); o_sts = dout("o_sts", [DEPTH, 4, 64, 128])
    o_gv = dout("o_gv", [DEPTH, TS, 512])

    XT = dscr("XT", [D, NT]); ZT = dscr("ZT", [33 * 128, NT]); ZK = dscr("ZK", [NT, ZK_W])
    MIXT = dscr("MIXT", [D, NT], BF16)
    BIASP = dscr("BIASP", [128, 16 * 256]); BIASS = dscr("BIASS", [16, 16 * 144])
    CCI = [nc.dram_tensor("CCI%d" % l, [128, CCW], F32) for l in range(DEPTH)]
    CCO = [nc.dram_tensor("CCO%d" % l, [8 * 128, CCW], F32) for l in range(DEPTH)]
    d_XT = [Buf("dXT%d" % i) for i in range(4)]
    d_Z = [Buf("dZ%d" % i) for i in range(4)]
    d_MIX = [Buf("dMIX%d" % i) for i in range(4)]
    d_misc = Buf("dmisc")
    d_cc = Buf("dcc")

    arena = nc.alloc_sbuf_tensor("arena", [128, NCELL * CELL], F32)
    cells = [Buf("cell%d" % i) for i in range(NCELL)]
    XR, R1, U = 0, 16, 32

    def fv(c0, n):
        return View(arena[:, c0 * CELL:(c0 + n) * CELL], cells[c0:c0 + n])

    def bv(c0, n):
        return View(arena[:, c0 * CELL:(c0 + n) * CELL].bitcast(BF16), cells[c0:c0 + n])

    wslots = []
    for i in range(3):
        t = nc.alloc_sbuf_tensor("wslot%d" % i, [128, 8192], BF16)
        wslots.append(View(t[:], [Buf("wslot%d" % i)]))
    wsmall = nc.alloc_sbuf_tensor("wsmall", [128, 16, 16], BF16)
    wsmall_v = View(wsmall[:], [Buf("wsmall")])
    wctr = [0]

    def wload(src_ap, shape_str=None, **kw):
        slot = wslots[wctr[0] % 3]
        wctr[0] += 1
        n = 1
        for s in src_ap.shape[1:]:
            n *= s
        dst = slot.ap[:, 0:n]
        if len(src_ap.shape) == 3:
            dst = dst.rearrange("p (a b) -> p a b", a=src_ap.shape[1])
        T.dma("pool", dst, src_ap, reads=[], writes=slot.bufs, owner=slot.bufs[0])
        return View(dst, slot.bufs)

    def sb(name, shape, dt=F32):
        t = nc.alloc_sbuf_tensor("s_" + name, list(shape), dt)
        return View(t[:], [Buf(name)])

    ident = sb("ident", [128, 128]); identb = sb("identb", [128, 128], BF16); onesb = sb("onesb", [128, 128], BF16)
    onesf = sb("onesf", [128, 128])
    epsb = sb("epsb", [128, 1])
    gain = [sb("gain%d" % i, [128, DEPTH, 16]) for i in range(4)]
    gon = sb("gon", [128, DEPTH, 4])
    triu = sb("triu", [64, 64]); tris = sb("tris", [64, 64]); cmask = sb("cmask", [64, 4, 64])
    gmask = sb("gmask", [128, 128]); sel = sb("sel", [128, 16])
    gvn_b = sb("gvn_b", [128, 512]); bsp_row = sb("bsp_row", [1, 512], BF16); bsp_f = sb("bsp_f", [1, 512])
    wg17 = sb("wg17", [17, 256]); sinks_b = sb("sinks_b", [128, 16])
    wmT = sb("wmT", [128, 4, 128], BF16)
    hb = sb("hb", [128, 1])
    small = sb("small", [128, 64])
    Pall = sb("Pall", [64, 4, 33])
    S_p = sb("S_p", [64, 4, 128]); S_s = sb("S_s", [64, 4, 128])
    S_pb = sb("S_pb", [64, 4, 128], BF16); S_sb = sb("S_sb", [64, 4, 128], BF16)
    S_st = sb("S_st", [64, 4, 128]); S_stb = sb("S_stb", [64, 4, 128], BF16)

    pst = [nc.alloc_psum_tensor("ps%d" % i, [128, 1024], F32) for i in range(4)]
    psl = [View(pst[i][:], [Buf("ps%d" % i)]) for i in range(4)]
    pctr = [0]

    def psum():
        v = psl[pctr[0] % 4]
        pctr[0] += 1
        return v

    def dve(fn, reads, writes):
        return T.op("dve", fn, [b for v in reads for b in v.bufs], [b for v in writes for b in v.bufs])

    def act(fn, reads, writes):
        return T.op("act", fn, [b for v in reads for b in v.bufs], [b for v in writes for b in v.bufs])

    def pe(fn, reads, writes, inc=True):
        return T.op("pe", fn, [b for v in reads for b in v.bufs], [b for v in writes for b in v.bufs], inc=inc)

    def load(dst, src_ap, dram_bufs=(), q="sp"):
        T.dma(q, dst.ap, src_ap, reads=list(dram_bufs), writes=dst.bufs, owner=dst.bufs[0])

    def store(dst_ap, src, dram_bufs=(), q="sp", is_output=False):
        T.dma(q, dst_ap, src.ap, reads=src.bufs, writes=list(dram_bufs), owner=src.bufs[0], is_output=is_output)

    def sub(v, ap):
        return View(ap, v.bufs)

    load(ident, c_ident[:, :]); load(triu, c_triu[:, :]); load(tris, c_tris[:, :]); load(cmask, c_cmask[:, :, :])
    load(gmask, c_gmask[:, :]); load(sel, c_sel[:, :])
    dve(lambda e: e.tensor_copy(out=identb.ap, in_=ident.ap), [ident], [identb])
    dve(lambda e: e.memset(onesb.ap, 1.0), [], [onesb])
    dve(lambda e: e.memset(onesf.ap, 1.0), [], [onesf])
    dve(lambda e: e.memset(epsb.ap, EPS), [], [epsb])
    with nc.allow_non_contiguous_dma(reason="tiny gain vectors"):
        for i, g in enumerate((g1, g2, g3, g4)):
            load(gain[i], g.rearrange("l (c p) -> p l c", p=128))
        load(gon, g_on.rearrange("l (h p) -> p l h", p=128))
    dve(lambda e: e.tensor_reduce(out=small.ap[:, 0:1], in_=sel.ap[:, 0:8], axis=AX.X, op=ALU.add), [sel], [small])
    dve(lambda e: e.tensor_scalar(out=hb.ap, in0=small.ap[:, 0:1], scalar1=-NEG, scalar2=NEG, op0=ALU.mult, op1=ALU.add), [small], [hb])

    trev = fv(U + 0, 1); rb = fv(U + 1, 1); smk = fv(U + 2, 1)
    trev_v = sub(trev, trev.ap[0:32, 0:383]); rb_v = sub(rb, rb.ap[0:32, 0:16]); smk_v = sub(smk, smk.ap[:, 0:256])
    load(trev_v, c_trev[:, :]); load(rb_v, rel_bias[:, :]); load(smk_v, c_smask[:, :])
    bp = fv(U + 4, 8)
    bp_v = sub(bp, bp.ap[:, 0:4096].rearrange("p (h j) -> p h j", h=16))
    for j0 in range(0, 256, 32):
        p = psum()
        for jj in range(32):
            j = j0 + jj
            pe(lambda e, j=j, jj=jj: e.matmul(p.ap[:, jj * 16:(jj + 1) * 16], lhsT=trev_v.ap[:, 255 - j:383 - j], rhs=rb_v.ap,
                                              start=True, stop=True), [trev, rb], [p], inc=(jj == 31))
        dve(lambda e, j0=j0: e.tensor_tensor(out=bp_v.ap[:, :, j0:j0 + 32],
                                             in0=p.ap[:, 0:512].rearrange("p (j h) -> p h j", h=16),
                                             in1=smk_v.ap[:, j0:j0 + 32].unsqueeze(1).to_broadcast([128, 16, 32]), op=ALU.add),
            [p, smk], [bp])
    store(BIASP[:, :], sub(bp, bp.ap[:, 0:4096]), [d_misc])
    bs = fv(U + 12, 5)
    bs_v = sub(bs, bs.ap[0:16, 0:2304].rearrange("p (h j) -> p h j", h=16))
    for j0 in range(0, 144, 32):
        nj = min(32, 144 - j0)
        p = psum()
        for jj in range(nj):
            j = j0 + jj
            pe(lambda e, j=j, jj=jj: e.matmul(p.ap[0:16, jj * 16:(jj + 1) * 16], lhsT=trev_v.ap[:, 255 - j:255 - j + 16], rhs=rb_v.ap,
                                              start=True, stop=True), [trev, rb], [p], inc=(jj == nj - 1))
        dve(lambda e, j0=j0, nj=nj: e.tensor_copy(out=bs_v.ap[:, :, j0:j0 + nj],
                                                  in_=p.ap[0:16, 0:nj * 16].rearrange("p (j h) -> p h j", h=16)), [p], [bs])
    store(BIASS[:, :], sub(bs, bs.ap[0:16, 0:2304]), [d_misc])

    tiles = [(i * TW, TW, 0) for i in range(3)] + [(3 * TW, TW, TS)]

    def rmsnorm_to(xT, gidx, l, ncol, outT, sq, rstd):
        act(lambda e: e.activation(out=sq.ap[:, :, 0:ncol], in_=xT.ap[:, :, 0:ncol], func=AF.Square), [xT], [sq])
        segs = [(0, min(ncol, TW))] + ([(TW, ncol)] if ncol > TW else [])
        p = psum()
        for (a, b) in segs:
            for kc in range(16):
                pe(lambda e, kc=kc, a=a, b=b: e.matmul(p.ap[:, a:b], lhsT=onesb.ap, rhs=sq.ap[:, kc, a:b], start=(kc == 0), stop=(kc == 15)),
                   [onesb, sq], [p], inc=(kc == 15))
        act(lambda e: e.activation(out=rstd.ap[:, 0:ncol], in_=p.ap[:, 0:ncol], func=AF.Sqrt, bias=epsb.ap, scale=1.0 / D), [p, epsb], [rstd])
        dve(lambda e: e.reciprocal(out=rstd.ap[:, 0:ncol], in_=rstd.ap[:, 0:ncol]), [rstd], [rstd])
        for kc in range(16):
            dve(lambda e, kc=kc: e.scalar_tensor_tensor(out=outT.ap[:, kc, 0:ncol], in0=xT.ap[:, kc, 0:ncol], scalar=gain[gidx].ap[:, l, kc:kc + 1],
                                                        in1=rstd.ap[:, 0:ncol], op0=ALU.mult, op1=ALU.mult), [xT, gain[gidx], rstd], [outT])
        return segs

    def norm_residual(xT, mo, gidx, l, ncol, sq, rstd):
        act(lambda e: e.activation(out=sq.ap[:, :, 0:ncol], in_=mo.ap[:, :, 0:ncol], func=AF.Square), [mo], [sq])
        segs = [(0, min(ncol, TW))] + ([(TW, ncol)] if ncol > TW else [])
        p = psum()
        for (a, b) in segs:
            for kc in range(16):
                pe(lambda e, kc=kc, a=a, b=b: e.matmul(p.ap[:, a:b], lhsT=onesb.ap, rhs=sq.ap[:, kc, a:b], start=(kc == 0), stop=(kc == 15)),
                   [onesb, sq], [p], inc=(kc == 15))
        act(lambda e: e.activation(out=rstd.ap[:, 0:ncol], in_=p.ap[:, 0:ncol], func=AF.Sqrt, bias=epsb.ap, scale=1.0 / D), [p, epsb], [rstd])
        dve(lambda e: e.reciprocal(out=rstd.ap[:, 0:ncol], in_=rstd.ap[:, 0:ncol]), [rstd], [rstd])
        for kc in range(16):
            dve(lambda e, kc=kc: e.scalar_tensor_tensor(out=mo.ap[:, kc, 0:ncol], in0=mo.ap[:, kc, 0:ncol], scalar=gain[gidx].ap[:, l, kc:kc + 1],
                                                        in1=rstd.ap[:, 0:ncol], op0=ALU.mult, op1=ALU.mult), [mo, gain[gidx], rstd], [mo])
        dve(lambda e: e.tensor_tensor(out=xT.ap[:, :, 0:ncol], in0=xT.ap[:, :, 0:ncol], in1=mo.ap[:, :, 0:ncol], op=ALU.add), [xT, mo], [xT])

    def colsegs(ncol):
        return [(0, min(ncol, TW))] + ([(TW, ncol)] if ncol > TW else [])

    def fm_group(p, wview, wcols, kcs, rhsT, ncol, extra_reads=()):
        segs = colsegs(ncol)
        nk = len(kcs)
        for si, (a, b) in enumerate(segs):
            for i, kc in enumerate(kcs):
                pe(lambda e, kc=kc, i=i, a=a, b=b: e.matmul(p.ap[0:wcols[1] - wcols[0], a:b], lhsT=wview.ap[:, kc, wcols[0]:wcols[1]],
                                                            rhs=rhsT.ap[:, kc, a:b], start=(i == 0), stop=(i == nk - 1)),
                   [wview, rhsT] + list(extra_reads), [p], inc=(i == nk - 1))

    ccsem = [nc.alloc_semaphore("ccsem%d" % i) for i in range(DEPTH)]

    def cf(c0, n, rows=128, cols=None):
        v = fv(c0, n)
        return sub(v, v.ap[0:rows, 0:(cols if cols is not None else n * CELL)])

    def cb(c0, n, rows=128, cols=None):
        v = bv(c0, n)
        return sub(v, v.ap[0:rows, 0:(cols if cols is not None else n * 2 * CELL)])

    def phase_b(l):
        load(gvn_b, g_vn[l:l + 1, :].broadcast_to([128, 512]))
        load(sinks_b, sinks[l:l + 1, :].broadcast_to([128, 16]))
        load(bsp_f, b_sp[l:l + 1].rearrange("o g n -> o (g n)"))
        dve(lambda e: e.tensor_copy(out=bsp_row.ap, in_=bsp_f.ap), [bsp_f], [bsp_row])
        load(sub(wg17, wg17.ap[0:16, :]), w_gate[l])
        load(sub(wg17, wg17.ap[16:17, :]), b_gate[l:l + 1, :])
        for g in range(4):
            ws = cf(U + 0 + (g % 2), 1, cols=128)
            load(ws, w_sp[l, g])
            dve(lambda e: e.tensor_tensor(out=ws.ap, in0=ws.ap, in1=gmask.ap, op=ALU.mult), [ws, gmask], [ws])
            p = psum()
            pe(lambda e: e.transpose(out=p.ap[:, 0:128], in_=ws.ap, identity=ident.ap), [ws, ident], [p])
            act(lambda e, g=g: e.activation(out=wmT.ap[:, g, :], in_=p.ap[:, 0:128], func=AF.Copy), [p], [wmT])
        if STOP <= 1:
            return

        oTg_f = fv(R1, 16)
        oTg = sub(oTg_f, oTg_f.ap[:, 0:4 * NT].rearrange("p (h t) -> p h t", h=4))
        qsg_b = cb(XR + 0, 8, rows=64, cols=4 * NT)
        qsg = sub(qsg_b, qsg_b.ap.rearrange("p (h t) -> p h t", h=4))

        def gla_tile(tok0, ntok, CS, nch, S, Sb, prompt, gc0):
            cg = cf(U + 0, 1, rows=17, cols=ntok)
            dve(lambda e: e.memset(cg.ap, 1.0), [], [cg])
            load(sub(cg, cg.ap[0:16, :]), ZT[3584:3600, tok0:tok0 + ntok], d_Z)
            lneg_f = cf(U + 1, 4, rows=CS, cols=2048)
            lneg = sub(lneg_f, lneg_f.ap.rearrange("p (c f) -> p c f", c=8))
            for c4 in range(0, nch, 4):
                n4 = min(4, nch - c4)
                p = psum()
                for c in range(n4):
                    pe(lambda e, c=c: e.matmul(p.ap[0:CS, c * 256:(c + 1) * 256], lhsT=cg.ap[:, (c4 + c) * CS:(c4 + c + 1) * CS], rhs=wg17.ap,
                                               start=True, stop=True), [cg, wg17], [p], inc=(c == n4 - 1))
                act(lambda e: e.activation(out=lneg_f.ap[:, c4 * 256:(c4 + n4) * 256], in_=p.ap[0:CS, 0:n4 * 256], func=AF.Exp, scale=-1.0), [p], [lneg_f])
                act(lambda e: e.activation(out=lneg_f.ap[:, c4 * 256:(c4 + n4) * 256], in_=lneg_f.ap[:, c4 * 256:(c4 + n4) * 256], func=AF.Ln,
                                           bias=onesf.ap[0:CS, 0:1], scale=1.0), [lneg_f, onesf], [lneg_f])
            eb_f = cf(U + 5, 4, rows=64, cols=2048); enb_f = cf(U + 9, 4, rows=64, cols=2048)
            eb = sub(eb_f, eb_f.ap.rearrange("p (h c t) -> p h c t", h=4, c=8))
            enb = sub(enb_f, enb_f.ap.rearrange("p (h c t) -> p h c t", h=4, c=8))
            for h2 in range(2):
                p = psum()
                pv = p.ap[0:64, :].rearrange("p (h c t) -> p h c t", h=2, c=8)
                for hh in range(2):
                    h = h2 * 2 + hh
                    for c in range(nch):
                        pe(lambda e, h=h, hh=hh, c=c: e.matmul(pv[:, hh, c, 0:CS], lhsT=lneg.ap[:, c, h * 64:(h + 1) * 64], rhs=triu.ap[0:CS, 0:CS],
                                                               start=True, stop=True), [lneg_f, triu], [p], inc=(hh == 1 and c == nch - 1))
                act(lambda e: e.activation(out=eb.ap[:, h2 * 2:h2 * 2 + 2, 0:nch, 0:CS], in_=pv[:, :, 0:nch, 0:CS], func=AF.Exp), [p], [eb_f])
                act(lambda e: e.activation(out=enb.ap[:, h2 * 2:h2 * 2 + 2, 0:nch, 0:CS], in_=pv[:, :, 0:nch, 0:CS], func=AF.Exp, scale=-1.0), [p], [enb_f])
            ebrev_f = cf(U + 13, 4, rows=CS, cols=2048)
            for c4 in range(0, nch, 4):
                n4 = min(4, nch - c4)
                p = psum()
                for c in range(n4):
                    pe(lambda e, c=c: e.matmul(p.ap[0:CS, c * 256:(c + 1) * 256], lhsT=tris.ap[0:CS, 0:CS], rhs=lneg.ap[:, c4 + c, :],
                                               start=True, stop=True), [lneg_f, tris], [p], inc=(c == n4 - 1))
                act(lambda e: e.activation(out=ebrev_f.ap[:, c4 * 256:(c4 + n4) * 256], in_=p.ap[0:CS, 0:n4 * 256], func=AF.Exp), [p], [ebrev_f])
            qst_f = cf(U + 17, 4, rows=64, cols=4 * ntok)
            qst = sub(qst_f, qst_f.ap.rearrange("p (h t) -> p h t", h=4))
            qt_b = cb(U + 21, 2, rows=64, cols=4 * ntok); kt_b = cb(U + 23, 2, rows=64, cols=4 * ntok)
            qt = sub(qt_b, qt_b.ap.rearrange("p (h t) -> p h t", h=4))
            kt = sub(kt_b, kt_b.ap.rearrange("p (h t) -> p h t", h=4))
            load(qst, ZT[2560:2816, tok0:tok0 + ntok].rearrange("(h p) c -> p h c", p=64), d_Z)
            dve(lambda e: e.scalar_tensor_tensor(out=qt.ap.rearrange("p h (c t) -> p h c t", c=nch), in0=qst.ap.rearrange("p h (c t) -> p h c t", c=nch),
                                                 scalar=0.125, in1=eb.ap[:, :, 0:nch, 0:CS], op0=ALU.mult, op1=ALU.mult), [qst_f, eb_f], [qt_b])
            load(qst, ZT[2816:3072, tok0:tok0 + ntok].rearrange("(h p) c -> p h c", p=64), d_Z)
            dve(lambda e: e.tensor_tensor(out=kt.ap.rearrange("p h (c t) -> p h c t", c=nch), in0=qst.ap.rearrange("p h (c t) -> p h c t", c=nch),
                                          in1=enb.ap[:, :, 0:nch, 0:CS], op=ALU.mult), [qst_f, enb_f], [kt_b])
            kst_f = cf(U + 25, 4, rows=CS, cols=nch * 256)
            load(sub(kst_f, kst_f.ap.rearrange("p (c f) -> p c f", c=nch)), ZK[tok0:tok0 + ntok, 1024:1280].rearrange("(c p) f -> p c f", p=CS), d_Z)
            kh_b = cb(U + 29, 2, rows=CS, cols=nch * 256)
            kh = sub(kh_b, kh_b.ap.rearrange("p (c f) -> p c f", c=nch))
            dve(lambda e: e.tensor_tensor(out=kh_b.ap, in0=kst_f.ap, in1=ebrev_f.ap[:, 0:nch * 256], op=ALU.mult), [kst_f, ebrev_f], [kh_b])
            vb_b = cb(U + 1, 4, rows=CS, cols=nch * 512)
            vb = sub(vb_b, vb_b.ap.rearrange("p (c f) -> p c f", c=nch))
            T.dma("pool", vb.ap, ZK[tok0:tok0 + ntok, 1280:1792].rearrange("(c p) f -> p c f", p=CS), reads=list(d_Z), writes=vb.bufs, owner=vb.bufs[0])
            attm_b = cb(U + 31, 1, rows=CS, cols=4 * CS)
            attm = sub(attm_b, attm_b.ap.rearrange("p (h t) -> p h t", h=4))
            for c in range(nch):
                ts_ = slice(c * CS, (c + 1) * CS)
                pa = psum()
                pav = pa.ap[0:CS, 0:4 * CS].rearrange("p (h t) -> p h t", h=4)
                for h in range(4):
                    pe(lambda e, h=h: e.matmul(pav[:, h, :], lhsT=kt.ap[:, h, ts_], rhs=qt.ap[:, h, ts_], start=True, stop=True),
                       [kt_b, qt_b], [pa], inc=(h == 3))
                dve(lambda e: e.tensor_tensor(out=attm.ap, in0=pav, in1=cmask.ap[0:CS, :, 0:CS], op=ALU.mult), [pa, cmask], [attm_b])
                pu = psum()
                puv = pu.ap[0:64, 0:512].rearrange("p (h v) -> p h v", h=4)
                for h in range(4):
                    pe(lambda e, h=h: e.matmul(puv[:, h, :], lhsT=kh.ap[:, c, h * 64:(h + 1) * 64], rhs=vb.ap[:, c, h * 128:(h + 1) * 128],
                                               start=True, stop=True), [kh_b, vb_b], [pu], inc=(h == 3))
                po = psum()
                pov = po.ap[:, 0:4 * CS].rearrange("p (h t) -> p h t", h=4)
                for h in range(4):
                    pe(lambda e, h=h: e.matmul(pov[:, h, :], lhsT=vb.ap[:, c, h * 128:(h + 1) * 128], rhs=attm.ap[:, h, :], start=True, stop=False),
                       [vb_b, attm_b], [po], inc=False)
                    pe(lambda e, h=h: e.matmul(pov[:, h, :], lhsT=Sb.ap[:, h, :], rhs=qt.ap[:, h, ts_], start=False, stop=True), [Sb, qt_b], [po], inc=(h == 3))
                act(lambda e: e.activation(out=oTg.ap[:, :, tok0 + c * CS:tok0 + (c + 1) * CS], in_=pov, func=AF.Copy), [po], [oTg_f])
                for h in range(4):
                    dve(lambda e, h=h: e.scalar_tensor_tensor(out=S.ap[:, h, :], in0=S.ap[:, h, :], scalar=eb.ap[:, h, c, CS - 1:CS], in1=puv[:, h, :],
                                                              op0=ALU.mult, op1=ALU.add), [S, eb_f, pu], [S])
                act(lambda e: e.activation(out=Sb.ap, in_=S.ap, func=AF.Copy), [S], [Sb])
                if prompt:
                    gc = gc0 + c
                    dve(lambda e, gc=gc: e.tensor_tensor(out=Pall.ap[:, :, gc + 1:gc + 2], in0=Pall.ap[:, :, gc:gc + 1], in1=eb.ap[:, :, c, CS - 1:CS], op=ALU.mult),
                        [Pall, eb_f], [Pall])
            if prompt:
                dve(lambda e: e.tensor_tensor(out=qsg.ap[:, :, tok0:tok0 + ntok].rearrange("p h (c t) -> p h c t", c=nch),
                                              in0=qt.ap.rearrange("p h (c t) -> p h c t", c=nch),
                                              in1=Pall.ap[:, :, gc0:gc0 + nch].unsqueeze(3).to_broadcast([64, 4, nch, CS]), op=ALU.mult),
                    [qt_b, Pall], [qsg_b])

        dve(lambda e: e.memset(S_p.ap, 0.0), [], [S_p])
        dve(lambda e: e.memset(S_pb.ap, 0.0), [], [S_pb])
        dve(lambda e: e.memset(Pall.ap, 1.0), [], [Pall])
        for ti in range(4):
            gla_tile(ti * TW, TW, 64, 8, S_p, S_pb, True, ti * 8)
        if STOP <= 2:
            return
        pay = cf(U + 17, 3, cols=CCW)
        dve(lambda e: e.memset(pay.ap, 0.0), [], [pay])
        load(sub(pay, pay.ap[0:64, 0:512].rearrange("p (k t) -> p k t", k=4)), ZT[1024:1280, TP - 128:TP].rearrange("(k p) c -> p k c", p=64), d_Z)
        load(sub(pay, pay.ap[:, 512:768]), ZK[TP - 128:TP, 256:512], d_Z)
        dve(lambda e: e.tensor_copy(out=pay.ap[0:64, 768:1280].rearrange("p (h v) -> p h v", h=4), in_=S_p.ap), [S_p], [pay])
        dve(lambda e: e.tensor_copy(out=pay.ap[0:64, 1280:1284], in_=Pall.ap[:, :, 32]), [Pall], [pay])
        store(CCI[l].ap()[:, :], pay, [d_cc])
        epool = T.E["pool"]
        T._waits(epool, [], [d_cc])
        for q_, pl in T.dpool.items():
            for sem_, val_ in zip(pl["sems"], pl["vals"]):
                if epool.seen.get(sem_, 0) < val_:
                    epool.h.wait_ge(sem_, val_)
                    epool.seen[sem_] = val_
        ins = nc.gpsimd.collective_compute("AllGather", ALU.bypass, replica_groups=[list(range(NCORES))],
                                           ins=[CCI[l].ap().opt()], outs=[CCO[l].ap().opt()])
        ins.then_inc(ccsem[l], 1)
        T._record((ccsem[l], 1), [], [d_cc])
        for q_ in ("pool", "sp"):
            eq = T.E[q_]
            eq.h.wait_ge(ccsem[l], 1)
            eq.seen[ccsem[l]] = 1
        if STOP <= 3:
            return
        load(S_s, sg[l].rearrange("h d v -> d h v"))
        act(lambda e: e.activation(out=S_sb.ap, in_=S_s.ap, func=AF.Copy), [S_s], [S_sb])
        gla_tile(TP, TS, TS, 1, S_s, S_sb, False, 0)
        store(o_sts[l].rearrange("h d v -> d h v"), S_s, is_output=True)
        if STOP <= 4:
            return

        def gelu_inplace(x, tmp):
            dve(lambda e: e.tensor_tensor(out=tmp.ap, in0=x.ap, in1=x.ap, op=ALU.mult), [x], [tmp])
            dve(lambda e: e.tensor_scalar(out=tmp.ap, in0=tmp.ap, scalar1=0.044715, scalar2=1.0, op0=ALU.mult, op1=ALU.add), [tmp], [tmp])
            dve(lambda e: e.tensor_tensor(out=tmp.ap, in0=tmp.ap, in1=x.ap, op=ALU.mult), [tmp, x], [tmp])
            act(lambda e: e.activation(out=tmp.ap, in_=tmp.ap, func=AF.Sigmoid, scale=1.5957691216057308), [tmp], [tmp])
            dve(lambda e: e.tensor_tensor(out=x.ap, in0=x.ap, in1=tmp.ap, op=ALU.mult), [x, tmp], [x])

        def gmlp_tile(tok0, nblk, rows):
            ntok = nblk * rows
            bvt_f = cf(U + 0, 4, rows=rows, cols=nblk * 512)
            bvt = sub(bvt_f, bvt_f.ap.rearrange("p (b f) -> p b f", b=nblk))
            load(bvt, ZK[tok0:tok0 + ntok, 512:1024].rearrange("(b p) f -> p b f", p=rows), d_Z)
            but_f = cf(U + 4, 4, cols=4 * ntok)
            but = sub(but_f, but_f.ap.rearrange("p (g t) -> p g t", g=4))
            load(but, ZT[1536:2048, tok0:tok0 + ntok].rearrange("(g p) c -> p g c", p=128), d_Z)
            gelu_inplace(bvt_f, cf(U + 8, 4, rows=rows, cols=nblk * 512))
            gelu_inplace(but_f, cf(U + 12, 4, cols=4 * ntok))
            vn_b = cb(U + 16, 2, rows=rows, cols=nblk * 512)
            vn = sub(vn_b, vn_b.ap.rearrange("p (b f) -> p b f", b=nblk))
            vnf_f = cf(U + 8, 4, rows=rows, cols=nblk * 512)
            vnf = sub(vnf_f, vnf_f.ap.rearrange("p (b f) -> p b f", b=nblk))
            for b in range(nblk):
                dve(lambda e, b=b: e.bn_stats(out=small.ap[0:rows, 0:6], in_=bvt.ap[:, b, :]), [bvt_f], [small])
                dve(lambda e: e.bn_aggr(out=small.ap[0:rows, 8:10], in_=small.ap[0:rows, 0:6]), [small], [small])
                act(lambda e: e.activation(out=small.ap[0:rows, 10:11], in_=small.ap[0:rows, 9:10], func=AF.Sqrt, bias=epsb.ap[0:rows, :], scale=1.0), [small, epsb], [small])
                dve(lambda e: e.reciprocal(out=small.ap[0:rows, 10:11], in_=small.ap[0:rows, 10:11]), [small], [small])
                dve(lambda e, b=b: e.tensor_scalar(out=vnf.ap[:, b, :], in0=bvt.ap[:, b, :], scalar1=small.ap[0:rows, 8:9], scalar2=small.ap[0:rows, 10:11],
                                                   op0=ALU.subtract, op1=ALU.mult), [bvt_f, small], [vnf_f])
                dve(lambda e, b=b: e.tensor_tensor(out=vnf.ap[:, b, :], in0=vnf.ap[:, b, :], in1=gvn_b.ap[0:rows, :], op=ALU.mult), [vnf_f, gvn_b], [vnf_f])
            act(lambda e: e.activation(out=vn_b.ap, in_=vnf_f.ap, func=AF.Copy), [vnf_f], [vn_b])
            if rows == TS:
                store(o_gv[l], sub(vnf_f, vnf.ap[:, 0, :]), is_output=True)
            ob_b = cb(U + 18, 2, cols=4 * ntok)
            ob = sub(ob_b, ob_b.ap.rearrange("p (g t) -> p g t", g=4))
            for g in range(4):
                p = psum()
                for b in range(nblk):
                    pe(lambda e, g=g, b=b: e.matmul(p.ap[:, b * rows:(b + 1) * rows], lhsT=vn.ap[:, b, g * 128:(g + 1) * 128], rhs=wmT.ap[0:rows, g, 0:rows],
                                                    start=True, stop=False), [vn_b, wmT], [p], inc=False)
                    pe(lambda e, g=g, b=b: e.matmul(p.ap[:, b * rows:(b + 1) * rows], lhsT=onesb.ap[0:1, :], rhs=bsp_row.ap[0:1, g * 128:g * 128 + rows],
                                                    start=False, stop=True), [onesb, bsp_row], [p], inc=(b == nblk - 1))
                dve(lambda e, g=g: e.tensor_tensor(out=ob.ap[:, g, :], in0=p.ap[:, 0:ntok], in1=but.ap[:, g, :], op=ALU.mult), [p, but_f], [ob_b])
            ti = min(tok0 // TW, 3)
            store(MIXT[1024:1536, tok0:tok0 + ntok].rearrange("(g p) c -> p g c", p=128), ob, [d_MIX[ti]])

        for ti in range(4):
            gmlp_tile(ti * TW, 4, 128)
        gmlp_tile(TP, 1, TS)
        if STOP <= 5:
            return

        bp_f = cf(XR + 8, 8, cols=4096)
        biasp = sub(bp_f, bp_f.ap.rearrange("p (h j) -> p h j", h=16))
        load(bp_f, BIASP[:, :], [d_misc])
        bs_f = cf(U + 14, 5, rows=TS, cols=2304)
        biass = sub(bs_f, bs_f.ap.rearrange("p (h j) -> p h j", h=16))
        load(bs_f, BIASS[:, :], [d_misc])
        kTa_b = cb(U + 0, 9, rows=64, cols=4 * 2176)
        kTa = sub(kTa_b, kTa_b.ap.rearrange("p (k t) -> p k t", k=4))
        va_b = cb(U + 9, 5, cols=17 * 256)
        va = sub(va_b, va_b.ap.rearrange("p (b f) -> p b f", b=17))
        for kv in range(4):
            T.dma("pool", kTa.ap[:, kv, 128:2176], ZT[1024 + kv * 64:1024 + (kv + 1) * 64, 0:TP], reads=list(d_Z), writes=kTa.bufs, owner=kTa.bufs[0])
        T.dma("pool", va.ap[:, 1:17, :], ZK[0:TP, 256:512].rearrange("(b p) f -> p b f", p=128), reads=list(d_Z), writes=va.bufs, owner=va.bufs[0])
        kTs_b = cb(U + 29, 1, rows=64, cols=4 * 144)
        kTs = sub(kTs_b, kTs_b.ap.rearrange("p (k t) -> p k t", k=4))
        vs_b = cb(U + 30, 1, cols=2 * 256)
        vsm = sub(vs_b, vs_b.ap.rearrange("p (b f) -> p b f", b=2))
        ckt = cf(U + 19, 1, cols=256)
        load(ckt, ckc[l])
        p = psum()
        for kv in range(4):
            pe(lambda e, kv=kv: e.transpose(out=p.ap[0:64, kv * 128:(kv + 1) * 128], in_=ckt.ap[:, kv * 64:(kv + 1) * 64], identity=ident.ap),
               [ckt, ident], [p], inc=(kv == 3))
        act(lambda e: e.activation(out=kTs.ap[:, :, 0:128], in_=p.ap[0:64, 0:512].rearrange("p (k t) -> p k t", k=4), func=AF.Copy), [p], [kTs_b])
        for kv in range(4):
            T.dma("pool", kTs.ap[:, kv, 128:144], ZT[1024 + kv * 64:1024 + (kv + 1) * 64, TP:NT], reads=list(d_Z), writes=kTs.bufs, owner=kTs.bufs[0])
        T.dma("pool", vsm.ap[:, 0, :], cvc[l], reads=[], writes=vsm.bufs, owner=vsm.bufs[0])
        T.dma("pool", vsm.ap[0:TS, 1, :], ZK[TP:NT, 256:512], reads=list(d_Z), writes=vsm.bufs, owner=vsm.bufs[0])
        vfn_bufs = va.bufs + vsm.bufs

        def swa_block(tok0, nq, kTv, kcol0, kblocks, vfn, bias_v, halo, mix_ti):
            nk_tot = sum(kblocks)
            nkb = len(kblocks)
            qb_b = cb(U + 19 + 2 * ((tok0 // 128) % 2), 2, rows=64, cols=16 * nq)
            qb = sub(qb_b, qb_b.ap.rearrange("p (h t) -> p h t", h=16))
            T.dma("pool", qb.ap, ZT[0:1024, tok0:tok0 + nq].rearrange("(h p) c -> p h c", p=64), reads=list(d_Z), writes=qb.bufs, owner=qb.bufs[0])
            ot_b = cb(U + 27, 1, rows=nq, cols=1024)
            sm = small
            for g in range(4):
                ps_ = psum()
                psv = ps_.ap[0:nq, 0:1024].rearrange("p (h j) -> p h j", h=4)[:, :, 0:nk_tot]
                for j in range(4):
                    hq = g * 4 + j
                    pe(lambda e, j=j, hq=hq: e.matmul(psv[:, j, :], lhsT=qb.ap[:, hq, :], rhs=kTv.ap[:, g, kcol0:kcol0 + nk_tot], start=True, stop=True),
                       [qb_b, View(None, kTv.bufs)], [ps_], inc=(j == 3))
                s_f = cf(U + 23, 2, rows=nq, cols=4 * nk_tot)
                s3 = sub(s_f, s_f.ap.rearrange("p (h j) -> p h j", h=4))
                dve(lambda e, g=g: e.scalar_tensor_tensor(out=s3.ap, in0=psv, scalar=0.125, in1=bias_v.ap[0:nq, g * 4:(g + 1) * 4, 0:nk_tot],
                                                          op0=ALU.mult, op1=ALU.add), [ps_, bias_v], [s_f])
                if halo:
                    dve(lambda e: e.tensor_scalar(out=s3.ap[:, :, 0:128], in0=s3.ap[:, :, 0:128], scalar1=hb.ap[0:nq, 0:1], scalar2=None, op0=ALU.add), [s_f, hb], [s_f])
                dve(lambda e: e.tensor_reduce(out=sm.ap[0:nq, 16:20], in_=s3.ap, axis=AX.X, op=ALU.max), [s_f], [sm])
                dve(lambda e, g=g: e.tensor_tensor(out=sm.ap[0:nq, 16:20], in0=sm.ap[0:nq, 16:20], in1=sinks_b.ap[0:nq, g * 4:(g + 1) * 4], op=ALU.max), [sm, sinks_b], [sm])
                dve(lambda e: e.tensor_scalar(out=sm.ap[0:nq, 20:24], in0=sm.ap[0:nq, 16:20], scalar1=-1.0, scalar2=None, op0=ALU.mult), [sm], [sm])
                dve(lambda e: e.memset(sm.ap[0:nq, 24:28], 0.0), [], [sm])
                e_b = cb(U + 25, 1, rows=nq, cols=4 * nk_tot)
                e3 = sub(e_b, e_b.ap.rearrange("p (h j) -> p h j", h=4))
                for j in range(4):
                    act(lambda e, j=j: e.activation(out=e3.ap[:, j, :], in_=s3.ap[:, j, :], func=AF.Exp, bias=sm.ap[0:nq, 20 + j:21 + j], scale=1.0,
                                                    accum_out=sm.ap[0:nq, 24 + j:25 + j]), [s_f, sm], [e_b, sm])
                dve(lambda e, g=g: e.tensor_tensor(out=sm.ap[0:nq, 28:32], in0=sinks_b.ap[0:nq, g * 4:(g + 1) * 4], in1=sm.ap[0:nq, 16:20], op=ALU.subtract), [sm, sinks_b], [sm])
                act(lambda e: e.activation(out=sm.ap[0:nq, 28:32], in_=sm.ap[0:nq, 28:32], func=AF.Exp), [sm], [sm])
                dve(lambda e: e.tensor_tensor(out=sm.ap[0:nq, 28:32], in0=sm.ap[0:nq, 28:32], in1=sm.ap[0:nq, 24:28], op=ALU.add), [sm], [sm])
                dve(lambda e: e.reciprocal(out=sm.ap[0:nq, 32:36], in_=sm.ap[0:nq, 28:32]), [sm], [sm])
                pt = psum()
                ptb = pt.ap.bitcast(BF16)
                eT_b = cb(U + 26, 1, cols=8 * nq)
                for j in range(4):
                    ko = 0
                    for kb, nk in enumerate(kblocks):
                        last = (j == 3 and kb == nkb - 1)
                        pe(lambda e, j=j, kb=kb, nk=nk, ko=ko: e.transpose(out=ptb[0:nk, (j * nkb + kb) * nq:(j * nkb + kb + 1) * nq],
                                                                            in_=e3.ap[:, j, ko:ko + nk], identity=identb.ap[0:nq, 0:nq]),
                           [e_b, identb], [pt], inc=last)
                        ko += nk
                act(lambda e: e.activation(out=eT_b.ap[:, 0:4 * nkb * nq], in_=ptb[:, 0:4 * nkb * nq], func=AF.Copy), [pt], [eT_b])
                po = psum()
                pov = po.ap[0:nq, 0:256].rearrange("p (h d) -> p h d", h=4)
                for j in range(4):
                    for kb, nk in enumerate(kblocks):
                        pe(lambda e, j=j, kb=kb, nk=nk: e.matmul(pov[:, j, :], lhsT=eT_b.ap[0:nk, (j * nkb + kb) * nq:(j * nkb + kb + 1) * nq],
                                                                 rhs=vfn(kb)[:, g * 64:(g + 1) * 64], start=(kb == 0), stop=(kb == nkb - 1)),
                           [eT_b, View(None, vfn_bufs)], [po], inc=(j == 3 and kb == nkb - 1))
                dve(lambda e, g=g: e.tensor_tensor(out=ot_b.ap[:, g * 256:(g + 1) * 256].rearrange("p (h d) -> p h d", h=4), in0=pov,
                                                   in1=sm.ap[0:nq, 32:36].unsqueeze(2).to_broadcast([nq, 4, 64]), op=ALU.mult), [po, sm], [ot_b])
            pt = psum()
            ptb = pt.ap.bitcast(BF16)
            for c in range(8):
                pe(lambda e, c=c: e.transpose(out=ptb[:, c * nq:(c + 1) * nq], in_=ot_b.ap[:, c * 128:(c + 1) * 128], identity=identb.ap[0:nq, 0:nq]),
                   [ot_b, identb], [pt], inc=(c == 7))
            oa_b = cb(U + 28, 1, cols=8 * nq)
            act(lambda e: e.activation(out=oa_b.ap, in_=ptb[:, 0:8 * nq], func=AF.Copy), [pt], [oa_b])
            store(MIXT[0:1024, tok0:tok0 + nq].rearrange("(k p) c -> p k c", p=128), sub(oa_b, oa_b.ap.rearrange("p (k t) -> p k t", k=8)), [d_MIX[mix_ti]])

        if STOP <= 6:
            return
        for qb_i in range(1, 16):
            swa_block(qb_i * 128, 128, kTa, qb_i * 128, [128, 128], lambda kb, qb_i=qb_i: va.ap[:, qb_i + kb, :], biasp, False, qb_i // 4)
        if STOP <= 7:
            return
        swa_block(TP, TS, kTs, 0, [128, TS], lambda kb: (vsm.ap[:, 0, :] if kb == 0 else vsm.ap[0:TS, 1, :]), biass, False, 3)
        if STOP <= 8:
            return

        CCOv = CCO[l].ap().rearrange("(r p) c -> p r c", p=128)
        gkv_f = cf(U + 16, 12, cols=8 * 768)
        gkv = sub(gkv_f, gkv_f.ap.rearrange("p (r c) -> p r c", r=8))
        load(gkv, CCOv[:, :, 0:768], [d_cc])
        acc = cf(U + 14, 2, cols=768)
        dve(lambda e: e.tensor_scalar(out=acc.ap, in0=gkv.ap[:, 0, :], scalar1=sel.ap[:, 0:1], scalar2=None, op0=ALU.mult), [gkv_f, sel], [acc])
        for r in range(1, 8):
            dve(lambda e, r=r: e.scalar_tensor_tensor(out=acc.ap, in0=gkv.ap[:, r, :], scalar=sel.ap[:, r:r + 1], in1=acc.ap, op0=ALU.mult, op1=ALU.add),
                [gkv_f, sel, acc], [acc])
        act(lambda e: e.activation(out=kTa.ap[:, :, 0:128], in_=acc.ap[0:64, 0:512].rearrange("p (k t) -> p k t", k=4), func=AF.Copy), [acc], [kTa_b])
        act(lambda e: e.activation(out=va.ap[:, 0, :], in_=acc.ap[:, 512:768], func=AF.Copy), [acc], [va_b])
        gs_f = cf(U + 16, 8, rows=64, cols=8 * 516)
        gs = sub(gs_f, gs_f.ap.rearrange("p (r c) -> p r c", r=8))
        load(gs, CCOv[0:64, :, 768:1284], [d_cc])
        dve(lambda e: e.memset(S_st.ap, 0.0), [], [S_st])
        for r in range(8):
            dve(lambda e, r=r: e.tensor_scalar(out=small.ap[0:64, 40:44], in0=gs.ap[:, r, 512:516], scalar1=-1.0, scalar2=sel.ap[0:64, 8 + r:9 + r], op0=ALU.add, op1=ALU.mult),
                [gs_f, sel], [small])
            dve(lambda e: e.tensor_scalar(out=small.ap[0:64, 40:44], in0=small.ap[0:64, 40:44], scalar1=1.0, scalar2=None, op0=ALU.add), [small], [small])
            for h in range(4):
                dve(lambda e, h=h: e.tensor_scalar(out=S_st.ap[:, h, :], in0=S_st.ap[:, h, :], scalar1=small.ap[0:64, 40 + h:41 + h], scalar2=None, op0=ALU.mult), [S_st, small], [S_st])
                dve(lambda e, h=h, r=r: e.scalar_tensor_tensor(out=S_st.ap[:, h, :], in0=gs.ap[:, r, h * 128:(h + 1) * 128], scalar=sel.ap[0:64, 8 + r:9 + r],
                                                               in1=S_st.ap[:, h, :], op0=ALU.mult, op1=ALU.add), [gs_f, sel, S_st], [S_st])
        act(lambda e: e.activation(out=S_stb.ap, in_=S_st.ap, func=AF.Copy), [S_st], [S_stb])
        fin = cf(U + 14, 1, rows=64, cols=512)
        for h in range(4):
            dve(lambda e, h=h: e.scalar_tensor_tensor(out=fin.ap[:, h * 128:(h + 1) * 128], in0=S_st.ap[:, h, :], scalar=Pall.ap[:, h, 32:33], in1=S_p.ap[:, h, :],
                                                      op0=ALU.mult, op1=ALU.add), [S_st, Pall, S_p], [fin])
        store(o_stp[l].rearrange("h d v -> d h v"), sub(fin, fin.ap.rearrange("p (h v) -> p h v", h=4)), is_output=True)
        if STOP <= 9:
            return
        swa_block(0, 128, kTa, 0, [128, 128], lambda kb: va.ap[:, kb, :], biasp, True, 0)
        if STOP <= 10:
            return

        def gla_final(tok0, ntok, prompt, mix_ti):
            if prompt:
                for h in range(4):
                    p = psum()
                    pe(lambda e, h=h: e.matmul(p.ap[:, 0:ntok], lhsT=S_stb.ap[:, h, :], rhs=qsg.ap[:, h, tok0:tok0 + ntok], start=True, stop=True), [S_stb, qsg_b], [p])
                    dve(lambda e, h=h: e.tensor_tensor(out=oTg.ap[:, h, tok0:tok0 + ntok], in0=oTg.ap[:, h, tok0:tok0 + ntok], in1=p.ap[:, 0:ntok], op=ALU.add), [p, oTg_f], [oTg_f])
            sq_b = cb(U + 0, 2, cols=4 * ntok)
            sqv = sub(sq_b, sq_b.ap.rearrange("p (h t) -> p h t", h=4))
            act(lambda e: e.activation(out=sqv.ap, in_=oTg.ap[:, :, tok0:tok0 + ntok], func=AF.Square), [oTg_f], [sq_b])
            rs_f = cf(U + 2, 4, cols=4 * ntok)
            rs = sub(rs_f, rs_f.ap.rearrange("p (h t) -> p h t", h=4))
            for h in range(4):
                p = psum()
                pe(lambda e, h=h: e.matmul(p.ap[:, 0:ntok], lhsT=onesb.ap, rhs=sqv.ap[:, h, :], start=True, stop=True), [onesb, sq_b], [p])
                act(lambda e, h=h: e.activation(out=rs.ap[:, h, :], in_=p.ap[:, 0:ntok], func=AF.Sqrt, bias=epsb.ap, scale=1.0 / 128.0), [p, epsb], [rs_f])
            dve(lambda e: e.reciprocal(out=rs_f.ap, in_=rs_f.ap), [rs_f], [rs_f])
            for h in range(4):
                dve(lambda e, h=h: e.scalar_tensor_tensor(out=rs.ap[:, h, :], in0=oTg.ap[:, h, tok0:tok0 + ntok], scalar=gon.ap[:, l, h:h + 1], in1=rs.ap[:, h, :],
                                                          op0=ALU.mult, op1=ALU.mult), [oTg_f, gon, rs_f], [rs_f])
            co_f = cf(U + 6, 4, cols=4 * ntok)
            load(sub(co_f, co_f.ap.rearrange("p (h t) -> p h t", h=4)), ZT[3600:4112, tok0:tok0 + ntok].rearrange("(h p) c -> p h c", p=128), d_Z)
            sg_f = cf(U + 10, 4, cols=4 * ntok)
            act(lambda e: e.activation(out=sg_f.ap, in_=co_f.ap, func=AF.Sigmoid), [co_f], [sg_f])
            dve(lambda e: e.tensor_tensor(out=co_f.ap, in0=co_f.ap, in1=sg_f.ap, op=ALU.mult), [co_f, sg_f], [co_f])
            oc_b = cb(U + 14, 2, cols=4 * ntok)
            dve(lambda e: e.tensor_tensor(out=oc_b.ap, in0=rs_f.ap, in1=co_f.ap, op=ALU.mult), [rs_f, co_f], [oc_b])
            store(MIXT[1536:2048, tok0:tok0 + ntok].rearrange("(h p) c -> p h c", p=128), sub(oc_b, oc_b.ap.rearrange("p (h t) -> p h t", h=4)), [d_MIX[mix_ti]])

        for ti in range(4):
            gla_final(ti * TW, TW, True, ti)
        gla_final(TP, TS, False, 3)

    for l in range(depth):
        for ti, (c0, npc, nsc) in enumerate(tiles):
            ncol = npc + nsc
            xTf = fv(XR, 16)
            xT = sub(xTf, xTf.ap.rearrange("p (k c) -> p k c", k=16))
            hTb = bv(U + 0, 8)
            hT = sub(hTb, hTb.ap.rearrange("p (k c) -> p k c", k=16))
            sqb = bv(U + 16, 8)
            sq = sub(sqb, sqb.ap.rearrange("p (k c) -> p k c", k=16))
            rstd = fv(R1 + 0, 1)
            if l == 0:
                blocks = [(c0 + b * 128, 128, xp, b * 128) for b in range(4)] + ([(0, TS, xs, TW)] if nsc else [])
                for bi, (r0, nr, srcap, colo) in enumerate(blocks):
                    xtok = fv(U + 24 + 4 * (bi % 2), 4)
                    xtv = sub(xtok, xtok.ap[0:nr, 0:D])
                    load(xtv, srcap[r0:r0 + nr, :])
                    for k0 in range(0, 16, 4):
                        p = psum()
                        for kk in range(4):
                            kc = k0 + kk
                            pe(lambda e, kc=kc, kk=kk: e.transpose(out=p.ap[:, kk * 128:kk * 128 + nr], in_=xtv.ap[:, kc * 128:(kc + 1) * 128],
                                                                    identity=ident.ap[0:nr, 0:nr]), [xtok, ident], [p], inc=(kk == 3))
                        act(lambda e, k0=k0: e.activation(out=xT.ap[:, k0:k0 + 4, colo:colo + nr],
                                                          in_=p.ap[:, 0:512].rearrange("p (k c) -> p k c", k=4)[:, :, 0:nr], func=AF.Copy), [p], [xTf])
                store(XT.rearrange("(k p) c -> p k c", p=128)[:, :, c0:c0 + npc], sub(xTf, xT.ap[:, :, 0:npc]), [d_XT[ti]])
                if nsc:
                    store(XT.rearrange("(k p) c -> p k c", p=128)[:, :, TP:TP + nsc], sub(xTf, xT.ap[:, :, TW:TW + nsc]), [d_XT[ti]])
            else:
                load(sub(xTf, xT.ap[:, :, 0:npc]), XT.rearrange("(k p) c -> p k c", p=128)[:, :, c0:c0 + npc], [d_XT[ti]])
                if nsc:
                    load(sub(xTf, xT.ap[:, :, TW:TW + nsc]), XT.rearrange("(k p) c -> p k c", p=128)[:, :, TP:TP + nsc], [d_XT[ti]])
            rmsnorm_to(xT, 0, l, ncol, hT, sq, rstd)

            stg_ctr = [0]

            def fm_out(p, nrows, zrow0):
                s = fv(R1 + 1 + (stg_ctr[0] % 4), 1)
                stg_ctr[0] += 1
                act(lambda e: e.activation(out=s.ap[0:nrows, 0:ncol], in_=p.ap[0:nrows, 0:ncol], func=AF.Copy), [p], [s])
                store(ZT[zrow0:zrow0 + nrows, c0:c0 + npc], sub(s, s.ap[0:nrows, 0:npc]), [d_Z[ti]])
                if nsc:
                    store(ZT[zrow0:zrow0 + nrows, TP:TP + nsc], sub(s, s.ap[0:nrows, TW:TW + nsc]), [d_Z[ti]])

            tm_ctr = [0]

            def tm_out(wv, wc0, wc1, zk0):
                n = wc1 - wc0
                blocks = [(b * 128, 128, c0 + b * 128) for b in range(4)] + ([(TW, nsc, TP)] if nsc else [])
                for (col, nr, zr) in blocks:
                    p = psum()
                    for kc in range(16):
                        pe(lambda e, kc=kc: e.matmul(p.ap[0:nr, 0:n], lhsT=hT.ap[:, kc, col:col + nr], rhs=wv.ap[:, kc, wc0:wc1],
                                                     start=(kc == 0), stop=(kc == 15)), [hT, wv], [p], inc=(kc == 15))
                    s = fv(R1 + 5 + (tm_ctr[0] % 4), 1)
                    tm_ctr[0] += 1
                    dve(lambda e: e.tensor_copy(out=s.ap[0:nr, 0:n], in_=p.ap[0:nr, 0:n]), [p], [s])
                    store(ZK[zr:zr + nr, zk0:zk0 + n], sub(s, s.ap[0:nr, 0:n]), [d_Z[ti]])

            wsrc = w_in[l].rearrange("(k p) c -> p k c", p=128)
            for pi in range(7):
                wv = wload(wsrc[:, :, pi * 512:(pi + 1) * 512])
                if pi in (0, 1, 3):
                    for j in range(4):
                        p = psum()
                        fm_group(p, wv, (j * 128, (j + 1) * 128), range(16), hT, ncol)
                        fm_out(p, 128, pi * 512 + j * 128)
                elif pi == 2:
                    for j in range(2):
                        p = psum()
                        fm_group(p, wv, (j * 128, (j + 1) * 128), range(16), hT, ncol)
                        fm_out(p, 128, 1024 + j * 128)
                    tm_out(wv, 0, 512, 0)
                elif pi == 4:
                    tm_out(wv, 0, 512, 512)
                elif pi == 5:
                    for j in range(4):
                        p = psum()
                        fm_group(p, wv, (j * 128, (j + 1) * 128), range(16), hT, ncol)
                        fm_out(p, 128, 2560 + j * 128)
                    tm_out(wv, 256, 512, 1024)
                elif pi == 6:
                    tm_out(wv, 0, 512, 1280)
            T.dma("pool", wsmall_v.ap, wsrc[:, :, 3584:3600], reads=[], writes=wsmall_v.bufs, owner=wsmall_v.bufs[0])
            p = psum()
            fm_group(p, wsmall_v, (0, 16), range(16), hT, ncol)
            fm_out(p, 16, 3584)
            wv = wload(wsrc[:, :, 3600:4112])
            for j in range(4):
                p = psum()
                fm_group(p, wv, (j * 128, (j + 1) * 128), range(16), hT, ncol)
                fm_out(p, 128, 3600 + j * 128)

        kvs = fv(R1 + 9, 2)
        kv_p = sub(kvs, kvs.ap[:, 0:512]); kv_s = sub(kvs, kvs.ap[0:TS, 528:528 + 512])
        load(kv_p, ZK[TP - 128:TP, 0:512], [d_Z[3]])
        load(kv_s, ZK[TP:NT, 0:512], [d_Z[3]])
        store(o_kp[l], sub(kvs, kv_p.ap[:, 0:256]), is_output=True)
        store(o_vp[l], sub(kvs, kv_p.ap[:, 256:512]), is_output=True)
        store(o_ks[l], sub(kvs, kv_s.ap[:, 0:256]), is_output=True)
        store(o_vs[l], sub(kvs, kv_s.ap[:, 256:512]), is_output=True)

        phase_b(l)

        for ti, (c0, npc, nsc) in enumerate(tiles):
            ncol = npc + nsc
            segs = colsegs(ncol)
            xTf = fv(XR, 16)
            xT = sub(xTf, xTf.ap.rearrange("p (k c) -> p k c", k=16))
            mof = fv(R1, 16)
            mo = sub(mof, mof.ap.rearrange("p (k c) -> p k c", k=16))
            hBb = bv(R1, 8)
            hB = sub(hBb, hBb.ap.rearrange("p (k c) -> p k c", k=16))
            mixb = bv(U + 0, 8)
            mixT = sub(mixb, mixb.ap.rearrange("p (k c) -> p k c", k=16))
            sqb = bv(U + 8, 8)
            sq = sub(sqb, sqb.ap.rearrange("p (k c) -> p k c", k=16))
            rstd = fv(U + 16, 1)
            XTv = XT.rearrange("(k p) c -> p k c", p=128)
            MXv = MIXT.rearrange("(k p) c -> p k c", p=128)
            load(sub(xTf, xT.ap[:, :, 0:npc]), XTv[:, :, c0:c0 + npc], [d_XT[ti]])
            load(sub(mixb, mixT.ap[:, :, 0:npc]), MXv[:, :, c0:c0 + npc], [d_MIX[ti]])
            if nsc:
                load(sub(xTf, xT.ap[:, :, TW:TW + nsc]), XTv[:, :, TP:TP + nsc], [d_XT[ti]])
                load(sub(mixb, mixT.ap[:, :, TW:TW + nsc]), MXv[:, :, TP:TP + nsc], [d_MIX[ti]])
            wsrc = w_out[l].rearrange("(k p) c -> p k c", p=128)
            for pi in range(4):
                wv = wload(wsrc[:, :, pi * 512:(pi + 1) * 512])
                for j in range(4):
                    oc = pi * 4 + j
                    p = psum()
                    fm_group(p, wv, (j * 128, (j + 1) * 128), range(16), mixT, ncol)
                    act(lambda e, oc=oc: e.activation(out=mo.ap[:, oc, 0:ncol], in_=p.ap[:, 0:ncol], func=AF.Copy), [p], [View(None, [cells[R1 + oc]])])
            norm_residual(xT, mo, 1, l, ncol, sq, rstd)
            rmsnorm_to(xT, 2, l, ncol, hB, sq, rstd)
            wsrc = w_up[l].rearrange("(k p) c -> p k c", p=128)
            uTb = bv(U, 32)
            uT = sub(uTb, uTb.ap.rearrange("p (k c) -> p k c", k=64))
            for pi in range(16):
                wv = wload(wsrc[:, :, pi * 512:(pi + 1) * 512])
                for j in range(4):
                    fc = pi * 4 + j
                    p = psum()
                    fm_group(p, wv, (j * 128, (j + 1) * 128), range(16), hB, ncol)
                    rl = fv(R1 + 8 + (fc % 2), 1)
                    act(lambda e: e.activation(out=rl.ap[:, 0:ncol], in_=p.ap[:, 0:ncol], func=AF.Relu), [p], [rl])
                    dve(lambda e, fc=fc: e.tensor_tensor(out=uT.ap[:, fc, 0:ncol], in0=p.ap[:, 0:ncol], in1=rl.ap[:, 0:ncol], op=ALU.mult),
                        [p, rl], [View(None, [cells[U + fc // 2]])])
            wsrc = w_down[l].rearrange("(k p) c -> p k c", p=128)
            for oc in range(16):
                wv = wload(wsrc[:, :, oc * 128:(oc + 1) * 128])
                p = psum()
                for (a, b) in segs:
                    for fc in range(64):
                        pe(lambda e, fc=fc, a=a, b=b: e.matmul(p.ap[:, a:b], lhsT=wv.ap[:, fc, :], rhs=uT.ap[:, fc, a:b], start=(fc == 0), stop=(fc == 63)),
                           [wv, View(None, [cells[U + fc // 2]])], [p], inc=(fc == 63))
                act(lambda e, oc=oc: e.activation(out=mo.ap[:, oc, 0:ncol], in_=p.ap[:, 0:ncol], func=AF.Copy), [p], [View(None, [cells[R1 + oc]])])
            norm_residual(xT, mo, 3, l, ncol, sq, rstd)
            act(lambda e: e.activation(out=hB.ap[:, :, 0:ncol], in_=xT.ap[:, :, 0:ncol], func=AF.Copy), [xT], [hB])
            peTb = bv(U + 4, 1)
            peT = sub(peTb, peTb.ap.rearrange("p (k c) -> p k c", k=2))
            blocks = [(c0 + b * 128, 128, pp[l], b * 128) for b in range(4)] + ([(0, TS, psm[l], TW)] if nsc else [])
            for bi, (r0, nr, srcap, colo) in enumerate(blocks):
                ptok = fv(U + 5 + (bi % 2), 1)
                ptv = sub(ptok, ptok.ap[0:nr, 0:256])
                load(ptv, srcap[r0:r0 + nr, :])
                p = psum()
                for kk in range(2):
                    pe(lambda e, kk=kk: e.transpose(out=p.ap[:, kk * 128:kk * 128 + nr], in_=ptv.ap[:, kk * 128:(kk + 1) * 128],
                                                    identity=ident.ap[0:nr, 0:nr]), [ptok, ident], [p], inc=(kk == 1))
                act(lambda e: e.activation(out=peT.ap[:, :, colo:colo + nr], in_=p.ap[:, 0:256].rearrange("p (k c) -> p k c", k=2)[:, :, 0:nr],
                                           func=AF.Copy), [p], [peTb])
            wpleb = bv(U + 20, 4)
            wple = sub(wpleb, wpleb.ap[:, 0:4096].rearrange("p (a b) -> p a b", a=2))
            T.dma("pool", wple.ap, w_ple[l].rearrange("(k p) c -> p k c", p=128), reads=[], writes=wple.bufs, owner=wple.bufs[0])
            wsrc = w_pg[l].rearrange("(k p) c -> p k c", p=128)
            for pi in range(4):
                wv = wload(wsrc[:, :, pi * 512:(pi + 1) * 512])
                for j in range(4):
                    oc = pi * 4 + j
                    pg = psum()
                    fm_group(pg, wv, (j * 128, (j + 1) * 128), range(16), hB, ncol)
                    sg_ = fv(U + 8 + (oc % 2), 1)
                    act(lambda e: e.activation(out=sg_.ap[:, 0:ncol], in_=pg.ap[:, 0:ncol], func=AF.Sigmoid), [pg], [sg_])
                    pq = psum()
                    fm_group(pq, wple, (oc * 128, (oc + 1) * 128), range(2), peT, ncol)
                    dve(lambda e: e.tensor_tensor(out=sg_.ap[:, 0:ncol], in0=pq.ap[:, 0:ncol], in1=sg_.ap[:, 0:ncol], op=ALU.mult), [pq, sg_], [sg_])
                    dve(lambda e, oc=oc: e.tensor_tensor(out=xT.ap[:, oc, 0:ncol], in0=xT.ap[:, oc, 0:ncol], in1=sg_.ap[:, 0:ncol], op=ALU.add),
                        [sg_, View(None, [cells[XR + oc]])], [View(None, [cells[XR + oc]])])
            if l < DEPTH - 1:
                store(XTv[:, :, c0:c0 + npc], sub(xTf, xT.ap[:, :, 0:npc]), [d_XT[ti]])
                if nsc:
                    store(XTv[:, :, TP:TP + nsc], sub(xTf, xT.ap[:, :, TW:TW + nsc]), [d_XT[ti]])
            if l == depth - 1:
                blocks = [(c0 + b * 128, 128, y_p, b * 128) for b in range(4)] + ([(0, TS, y_s, TW)] if nsc else [])
                for bi, (r0, nr, dstap, colo) in enumerate(blocks):
                    ytok = fv(U + 10 + 4 * (bi % 2), 4)
                    for k0 in range(0, 16, 4):
                        p = psum()
                        for kk in range(4):
                            kc = k0 + kk
                            pe(lambda e, kc=kc, kk=kk: e.transpose(out=p.ap[0:nr, kk * 128:(kk + 1) * 128], in_=xT.ap[:, kc, colo:colo + nr],
                                                                    identity=ident.ap), [xTf, ident], [p], inc=(kk == 3))
                        act(lambda e, k0=k0: e.activation(out=ytok.ap[0:nr, k0 * 128:(k0 + 4) * 128], in_=p.ap[0:nr, 0:512], func=AF.Copy), [p], [ytok])
                    store(dstap[r0:r0 + nr, :], sub(ytok, ytok.ap[0:nr, 0:D]), is_output=True)

    T.finish()
    return nc


_NC_CACHE = {}


def kernel(x_prompt, x_sample, cache_swa_k, cache_swa_v, state_gla, p_prompt, p_sample,
           w_in, w_gate, b_gate, rel_bias, attn_sinks, w_spatial, b_spatial,
           g_gmlp_vnorm, g_gla_onorm, w_out, g_mix_pre, g_mix_post, g_ffn_pre,
           g_ffn_post, w_up, w_down, w_ple, w_ple_gate, _depth=DEPTH):
    f = lambda a: np.ascontiguousarray(np.asarray(a, dtype=np.float32))
    if _depth not in _NC_CACHE:
        _NC_CACHE[_depth] = build(_depth)
    nc = _NC_CACHE[_depth]
    consts = host_constants()
    shared = dict(consts)
    shared.update(w_in=f(w_in), w_gate=f(w_gate), b_gate=f(b_gate), rel_bias=f(rel_bias), sinks=f(attn_sinks),
                  w_sp=f(w_spatial), b_sp=f(b_spatial), g_vn=f(g_gmlp_vnorm), g_on=f(g_gla_onorm), w_out=f(w_out),
                  g1=f(g_mix_pre), g2=f(g_mix_post), g3=f(g_ffn_pre), g4=f(g_ffn_post), w_up=f(w_up), w_down=f(w_down),
                  w_ple=f(w_ple), w_pg=f(w_ple_gate))
    x_prompt = f(x_prompt); x_sample = f(x_sample); p_prompt = f(p_prompt); p_sample = f(p_sample)
    cache_swa_k = f(cache_swa_k); cache_swa_v = f(cache_swa_v); state_gla = f(state_gla)
    in_maps = []
    for c in range(NCORES):
        seq, seg = c // 4, c % 4
        m = dict(shared)
        sel = np.zeros((128, 16), np.float32)
        if seg > 0:
            sel[:, c - 1] = 1.0
        for r in range(seq * 4, c):
            sel[:, 8 + r] = 1.0
        m["sel"] = sel
        m["xp"] = np.ascontiguousarray(x_prompt[seq, seg * TP:(seg + 1) * TP])
        m["xs"] = np.ascontiguousarray(x_sample[c])
        m["pp"] = np.ascontiguousarray(p_prompt[:, seq, seg * TP:(seg + 1) * TP])
        m["psm"] = np.ascontiguousarray(p_sample[:, c])
        m["ckc"] = np.ascontiguousarray(cache_swa_k[:, c].reshape(DEPTH, 128, 256))
        m["cvc"] = np.ascontiguousarray(cache_swa_v[:, c].reshape(DEPTH, 128, 256))
        m["sg"] = np.ascontiguousarray(state_gla[:, c])
        in_maps.append(m)
    res = run_bass_kernel_spmd(nc, in_maps, core_ids=list(range(NCORES)))
    R = res.results
    y_prompt = np.zeros((2, 8192, D), np.float32)
    for c in range(NCORES):
        y_prompt[c // 4, (c % 4) * TP:(c % 4 + 1) * TP] = R[c]["y_p"]
    y_sample = np.stack([R[c]["y_s"] for c in range(NCORES)], 0)
    kp = np.stack([R[3]["o_kp"], R[7]["o_kp"]], 1).reshape(DEPTH, 2, 128, 4, 64)
    vp = np.stack([R[3]["o_vp"], R[7]["o_vp"]], 1).reshape(DEPTH, 2, 128, 4, 64)
    ks = np.stack([R[c]["o_ks"] for c in range(NCORES)], 1).reshape(DEPTH, 8, TS, 4, 64)
    vs = np.stack([R[c]["o_vs"] for c in range(NCORES)], 1).reshape(DEPTH, 8, TS, 4, 64)
    stp = np.stack([R[3]["o_stp"], R[7]["o_stp"]], 1)
    sts = np.stack([R[c]["o_sts"] for c in range(NCORES)], 1)
    gv = np.stack([R[c]["o_gv"] for c in range(NCORES)], 1)
    return (y_prompt, y_sample, kp.astype(np.float32), vp.astype(np.float32), ks.astype(np.float32), vs.astype(np.float32),
            stp.astype(np.float32), sts.astype(np.float32), gv.astype(np.float32))
```
